# Optimizing a Trainium2 kernel written in Bass

```python
import math
import jax
import jax.numpy as jnp
from jax import lax
import numpy as np

D_MODEL = 2048
BATCH = 1
SEQ = 8192
DEPTH = 4

CHUNK = 64
Q_BLOCK = 128
HEAD_DIM = 64
N_HEADS_DIFF = D_MODEL // (4 * HEAD_DIM)
N_HEADS_SB = D_MODEL // (4 * HEAD_DIM)
N_HEADS_CH = D_MODEL // (4 * HEAD_DIM)
DIFF_WIDTH = N_HEADS_DIFF * 2 * HEAD_DIM
SB_WIDTH = N_HEADS_SB * HEAD_DIM
CH_WIDTH = N_HEADS_CH * HEAD_DIM
MIX_WIDTH = DIFF_WIDTH + SB_WIDTH + CH_WIDTH
QKV_WIDTH = 3 * MIX_WIDTH
LEFT_CHUNKS = 8
BAND = (LEFT_CHUNKS + 1) * CHUNK
MAX_REL = 128
N_REL = (CHUNK - 1) + MAX_REL + 1
D_FF = 5632
CONV_WIDTH = 3
NORM_EPS = 1e-6
SUBLN_EPS = 1e-5

kernel_name = 'hybrid_chunk_causal_diff_sb_band_encoder'


def rmsnorm(x, g, eps=NORM_EPS):
    xf = x.astype(jnp.float32)
    y = xf * lax.rsqrt(jnp.mean(xf * xf, axis=-1, keepdims=True) + eps)
    return (y * g.astype(jnp.float32)).astype(x.dtype)


def alibi_slopes(n):
    return 2.0 ** (-8.0 * jnp.arange(1, n + 1, dtype=jnp.float32) / n)


def to_query_blocks(t):
    b, h, s = t.shape[:3]
    t = t.reshape((b, h, s // Q_BLOCK, Q_BLOCK) + t.shape[3:])
    return jnp.moveaxis(t, 2, 0)


def from_query_blocks(o):
    nb, b, h, qb, dv = o.shape
    return o.transpose(1, 0, 3, 2, 4).reshape(b, nb * qb, h * dv)


def split_columns(qkv):
    widths = [DIFF_WIDTH] * 3 + [SB_WIDTH] * 3 + [CH_WIDTH] * 3
    outs, start = [], 0
    for w in widths:
        outs.append(qkv[..., start:start + w])
        start += w
    return outs


def diff_attention(q, k, v, lam, lam_init, subln_g):
    b, s, _ = q.shape
    h = N_HEADS_DIFF
    q = q.reshape(b, s, h, 2, HEAD_DIM).transpose(0, 2, 1, 3, 4)
    k = k.reshape(b, s, h, 2, HEAD_DIM).transpose(0, 2, 1, 3, 4)
    v = v.reshape(b, s, h, 2 * HEAD_DIM).transpose(0, 2, 1, 3)
    slopes = alibi_slopes(h)
    k_pos = jnp.arange(s)
    k_chunk = k_pos // CHUNK
    scale = HEAD_DIM ** -0.5

    def block(args):
        q_blk, i = args
        q_pos = i * Q_BLOCK + jnp.arange(Q_BLOCK)
        sc = jnp.einsum('bhqcd,bhkcd->bhcqk', q_blk, k).astype(jnp.float32) * scale
        dist = jnp.abs(q_pos[:, None] - k_pos[None, :]).astype(jnp.float32)
        bias = -slopes[:, None, None] * dist
        allowed = k_chunk[None, :] <= (q_pos // CHUNK)[:, None]
        sc = jnp.where(allowed, sc + bias[None, :, None], -jnp.inf)
        p = jax.nn.softmax(sc, axis=-1)
        w = (p[:, :, 0] - lam * p[:, :, 1]).astype(v.dtype)
        return jnp.einsum('bhqk,bhkv->bhqv', w, v)

    o = lax.map(block, (to_query_blocks(q), jnp.arange(s // Q_BLOCK)))
    o = rmsnorm(o, subln_g, SUBLN_EPS) * (1.0 - lam_init)
    return from_query_blocks(o)


def stick_breaking_attention(q, k, v, head_g):
    b, s, _ = q.shape
    h = N_HEADS_SB
    q = q.reshape(b, s, h, HEAD_DIM).transpose(0, 2, 1, 3)
    k = k.reshape(b, s, h, HEAD_DIM).transpose(0, 2, 1, 3)
    v = v.reshape(b, s, h, HEAD_DIM).transpose(0, 2, 1, 3)
    k_pos = jnp.arange(s)
    scale = HEAD_DIM ** -0.5

    def block(args):
        q_blk, i = args
        q_pos = i * Q_BLOCK + jnp.arange(Q_BLOCK)
        z = jnp.einsum('bhqd,bhkd->bhqk', q_blk, k).astype(jnp.float32) * scale
        causal = k_pos[None, :] < q_pos[:, None]
        log_1m_beta = jnp.where(causal, -jax.nn.softplus(z), 0.0)
        suffix = lax.cumsum(log_1m_beta, axis=3, reverse=True) - log_1m_beta
        log_a = jax.nn.log_sigmoid(z) + suffix
        a = jnp.where(causal, jnp.exp(log_a), 0.0).astype(v.dtype)
        return jnp.einsum('bhqk,bhkd->bhqd', a, v)

    o = lax.map(block, (to_query_blocks(q), jnp.arange(s // Q_BLOCK)))
    o = rmsnorm(o, head_g[:, None, :])
    return from_query_blocks(o)


def chunk_band_attention(q, k, v, rel_table, head_g):
    b, s, _ = q.shape
    h = N_HEADS_CH
    nc = s // CHUNK
    pad = LEFT_CHUNKS * CHUNK
    q = q.reshape(b, s, h, HEAD_DIM).transpose(0, 2, 1, 3)
    k = k.reshape(b, s, h, HEAD_DIM).transpose(0, 2, 1, 3)
    v = v.reshape(b, s, h, HEAD_DIM).transpose(0, 2, 1, 3)
    qc = q.reshape(b, h, nc, CHUNK, HEAD_DIM)

    def band(t):
        tp = jnp.pad(t, ((0, 0), (0, 0), (pad, 0), (0, 0)))
        tp = tp.reshape(b, h, nc + LEFT_CHUNKS, CHUNK, HEAD_DIM)
        return jnp.concatenate([tp[:, :, o:o + nc] for o in range(LEFT_CHUNKS + 1)], axis=3)

    kb, vb = band(k), band(v)
    sc = jnp.einsum('bhcqd,bhckd->bhcqk', qc, kb).astype(jnp.float32) * HEAD_DIM ** -0.5
    q_loc = jnp.arange(CHUNK)
    k_loc = jnp.arange(BAND) - pad
    rel_idx = jnp.clip(q_loc[:, None] - k_loc[None, :], -(CHUNK - 1), MAX_REL) + (CHUNK - 1)
    bias = rel_table[:, rel_idx].astype(jnp.float32)
    valid = (jnp.arange(nc)[:, None] * CHUNK + k_loc[None, :]) >= 0
    sc = jnp.where(valid[:, None, :], sc + bias[:, None], -jnp.inf)
    p = jax.nn.softmax(sc, axis=-1).astype(vb.dtype)
    o = jnp.einsum('bhcqk,bhckd->bhcqd', p, vb).reshape(b, h, s, HEAD_DIM)
    o = rmsnorm(o, head_g[:, None, :])
    return o.transpose(0, 2, 1, 3).reshape(b, s, h * HEAD_DIM)


def causal_dwconv(hid, w, bias):
    s = hid.shape[1]
    hp = jnp.pad(hid, ((0, 0), (CONV_WIDTH - 1, 0), (0, 0)))
    out = bias
    for j in range(CONV_WIDTH):
        out = out + hp[:, j:j + s] * w[j]
    return out


def setup_inputs(seed: int = 0) -> dict:
    key = jax.random.key(seed)
    ks = jax.random.split(key, 20)

    def nrm(k, shape, scale):
        return jax.random.normal(k, shape, jnp.float32) * scale

    def gain(k, shape):
        return 1.0 + nrm(k, shape, 0.05)

    return {
        'x': nrm(ks[0], (BATCH, SEQ, D_MODEL), 1.0),
        'attn_norm': gain(ks[1], (DEPTH, D_MODEL)),
        'w_qkv': nrm(ks[2], (DEPTH, D_MODEL, QKV_WIDTH), D_MODEL ** -0.5),
        'lambda_q1': nrm(ks[3], (DEPTH, HEAD_DIM), 0.1),
        'lambda_k1': nrm(ks[4], (DEPTH, HEAD_DIM), 0.1),
        'lambda_q2': nrm(ks[5], (DEPTH, HEAD_DIM), 0.1),
        'lambda_k2': nrm(ks[6], (DEPTH, HEAD_DIM), 0.1),
        'diff_subln': gain(ks[7], (DEPTH, 2 * HEAD_DIM)),
        'sb_norm': gain(ks[8], (DEPTH, N_HEADS_SB, HEAD_DIM)),
        'rel_bias': nrm(ks[9], (DEPTH, N_HEADS_CH, N_REL), 0.2),
        'ch_norm': gain(ks[10], (DEPTH, N_HEADS_CH, HEAD_DIM)),
        'w_o': nrm(ks[11], (DEPTH, MIX_WIDTH, D_MODEL), MIX_WIDTH ** -0.5),
        'ffn_norm': gain(ks[12], (DEPTH, D_MODEL)),
        'w_gate': nrm(ks[13], (DEPTH, D_MODEL, D_FF), D_MODEL ** -0.5),
        'w_up': nrm(ks[14], (DEPTH, D_MODEL, D_FF), D_MODEL ** -0.5),
        'conv_w': nrm(ks[15], (DEPTH, CONV_WIDTH, D_FF), CONV_WIDTH ** -0.5),
        'conv_b': nrm(ks[16], (DEPTH, D_FF), 0.02),
        'w_down': nrm(ks[17], (DEPTH, D_FF, D_MODEL), D_FF ** -0.5),
        'final_norm': gain(ks[18], (D_MODEL,)),
    }


def reference(x, attn_norm, w_qkv, lambda_q1, lambda_k1, lambda_q2, lambda_k2,
              diff_subln, sb_norm, rel_bias, ch_norm, w_o, ffn_norm, w_gate, w_up,
              conv_w, conv_b, w_down, final_norm):
    for l in range(DEPTH):
        lam_init = 0.8 - 0.6 * math.exp(-0.3 * l)
        h = rmsnorm(x, attn_norm[l])
        qkv = jnp.einsum('bsd,de->bse', h, w_qkv[l])
        q_a, k_a, v_a, q_b, k_b, v_b, q_c, k_c, v_c = split_columns(qkv)
        lam = (jnp.exp(jnp.sum(lambda_q1[l] * lambda_k1[l]).astype(jnp.float32))
               - jnp.exp(jnp.sum(lambda_q2[l] * lambda_k2[l]).astype(jnp.float32))
               + lam_init)
        y_a = diff_attention(q_a, k_a, v_a, lam, lam_init, diff_subln[l])
        y_b = stick_breaking_attention(q_b, k_b, v_b, sb_norm[l])
        y_c = chunk_band_attention(q_c, k_c, v_c, rel_bias[l], ch_norm[l])
        y = jnp.concatenate([y_a, y_b, y_c], axis=-1)
        x = x + jnp.einsum('bse,ed->bsd', y, w_o[l])
        h = rmsnorm(x, ffn_norm[l])
        g = causal_dwconv(jnp.einsum('bsd,df->bsf', h, w_gate[l]), conv_w[l], conv_b[l])
        u = jnp.einsum('bsd,df->bsf', h, w_up[l])
        x = x + jnp.einsum('bsf,fd->bsd', jax.nn.silu(g) * u, w_down[l])
    return rmsnorm(x, final_norm)
```

```python
import contextlib
import math
import numpy as np
import ml_dtypes
import concourse.bass as bass
import concourse.mybir as mybir
from concourse.bass_utils import run_bass_kernel_spmd

F32 = mybir.dt.float32
BF16 = mybir.dt.bfloat16
AF = mybir.ActivationFunctionType
ALU = mybir.AluOpType
NPBF16 = ml_dtypes.bfloat16

D_MODEL = 2048
SEQ = 8192
DEPTH = 4
NCORES = 8
TOK = SEQ // NCORES
KC = D_MODEL // 128
D_FF = 5632
FC = D_FF // 128
NORM_EPS = 1e-6
SUBLN_EPS = 1e-5

ENGS = ("pe", "act", "dve", "pool", "sp")
EPOCH = 8000

class T:
    __slots__ = ("name", "writers", "readers")

    def __init__(self, name):
        self.name = name
        self.writers = []
        self.readers = []


class Op:
    __slots__ = ("eng", "fn", "deps", "is_dma", "key", "cnt", "needs_sig", "sig", "idx")


class Sched:
    def __init__(self, nc, same_engine_sync=True):
        self.nc = nc
        self.ops = {e: [] for e in ENGS}
        self.key_cnt = {}
        self.same = same_engine_sync
        self.final = []

    @staticmethod
    def _prune(lst):
        out = []
        seen = set()
        for o in reversed(lst):
            if o.is_dma:
                out.append(o)
            elif o.eng not in seen:
                seen.add(o.eng)
                out.append(o)
        out.reverse()
        return out

    def add(self, eng, fn, reads=(), writes=(), dma=None):
        op = Op()
        op.eng = eng
        op.fn = fn
        op.is_dma = dma is not None
        op.key = dma
        op.needs_sig = False
        op.sig = None
        deps = []
        for t in reads:
            deps.extend(t.writers)
        for t in writes:
            deps.extend(t.readers)
            deps.extend(t.writers)
        op.deps = deps
        if op.is_dma:
            c = self.key_cnt.get(dma, 0) + 16
            self.key_cnt[dma] = c
            op.cnt = c
        for d in deps:
            if not d.is_dma and (d.eng != eng or (self.same and eng != "pe")):
                d.needs_sig = True
        for t in writes:
            if t.readers:
                t.writers = [op]
                t.readers = []
            else:
                t.writers = self._prune(t.writers + [op])
        wset = set(id(t) for t in writes)
        for t in reads:
            if id(t) not in wset:
                t.readers = self._prune(t.readers + [op])
        op.idx = len(self.ops[eng])
        self.ops[eng].append(op)
        return op

    def finish(self, ops):
        self.final.extend(ops)

    def emit(self):
        nc = self.nc
        n_sems = {}
        for e in ENGS:
            k = 0
            for op in self.ops[e]:
                if op.needs_sig and not op.is_dma:
                    op.sig = (e, k // EPOCH, k % EPOCH + 1)
                    k += 1
            n_sems[e] = (k + EPOCH - 1) // EPOCH
        import contextlib
        with contextlib.ExitStack() as st:
            sems = {}
            for e in ENGS:
                for j in range(n_sems[e]):
                    sems[(e, j)] = st.enter_context(nc.semaphore(f"s_{e}_{j}"))
            dsems = {}
            for k in self.key_cnt:
                dsems[k] = st.enter_context(nc.semaphore(f"d_{k}"))
            block = st.enter_context(nc.Block())
            ops = self.ops
            same = self.same
            final = self.final

            def run(ename, eng):
                waited = {}

                def wait_for(d):
                    if d.is_dma:
                        s, v = dsems[d.key], d.cnt
                        sk = ("d", d.key)
                    else:
                        if d.eng == ename and (not same or ename == "pe"):
                            return
                        s, v = sems[(d.sig[0], d.sig[1])], d.sig[2]
                        sk = (d.sig[0], d.sig[1])
                    if waited.get(sk, 0) >= v:
                        return
                    waited[sk] = v
                    eng.wait_ge(s, v)

                for op in ops[ename]:
                    best = {}
                    for d in op.deps:
                        if d.is_dma:
                            sk = ("d", d.key)
                            v = d.cnt
                        else:
                            if d.eng == ename and (not same or ename == "pe"):
                                continue
                            sk = (d.sig[0], d.sig[1])
                            v = d.sig[2]
                        if sk not in best or best[sk][0] < v:
                            best[sk] = (v, d)
                    for sk, (v, d) in best.items():
                        wait_for(d)
                    ins = op.fn(eng)
                    if op.is_dma:
                        ins.then_inc(dsems[op.key], 16)
                    elif op.sig is not None:
                        ins.then_inc(sems[(op.sig[0], op.sig[1])], 1)
                if ename == "sp":
                    for d in final:
                        wait_for(d)

            @block.tensor
            def _(eng):
                run("pe", eng)

            @block.scalar
            def _(eng):
                run("act", eng)

            @block.vector
            def _(eng):
                run("dve", eng)

            @block.gpsimd
            def _(eng):
                run("pool", eng)

            @block.sync
            def _(eng):
                run("sp", eng)


class Ctx:
    def __init__(self):
        self.nc = bass.Bass("TRN2", target_bir_lowering=False)
        self.st = contextlib.ExitStack()
        self.s = Sched(self.nc)
        self._n = 0

    def dram_in(self, name, shape, dt):
        return self.nc.dram_tensor(name, list(shape), dt, kind="ExternalInput").ap()

    def dram_out(self, name, shape, dt):
        return self.nc.dram_tensor(name, list(shape), dt, kind="ExternalOutput").ap()

    def sb(self, name, shape, dt):
        return self.st.enter_context(self.nc.sbuf_tensor(name, list(shape), dt))

    def ps(self, name, shape=(128, 512), dt=F32):
        return self.st.enter_context(self.nc.psum_tensor(name, list(shape), dt))

    def finish(self):
        self.s.emit()
        self.st.close()
        return self.nc


def _mm(out, lhsT, rhs, start, stop):
    return lambda e: e.matmul(out, lhsT, rhs, start=start, stop=stop)


def emit_rmsnorm(cx, x_sb, xt, g_sb, gt, h_sb, ht, ones_f, ones_t, ps_ss, ps_ss_t, sq, sq_t, rstd, rstd_t, eps, ntok=TOK):
    s = cx.s
    nth = ntok // 512
    for c in range(KC):
        for th in range(nth):
            sl = slice(th * 512, (th + 1) * 512)
            k = (c * nth + th) % 2
            s.add("act", (lambda e, o=sq[k][:, :], i=x_sb[:, c, sl]: e.activation(o, i, AF.Square)),
                  reads=[xt[c][th]], writes=[sq_t[k]])
            s.add("pe", _mm(ps_ss[th][:, :], ones_f[:, :], sq[k][:, :], c == 0, c == KC - 1),
                  reads=[sq_t[k], ones_t], writes=[ps_ss_t[th]])
    for th in range(nth):
        s.add("act", (lambda e, o=rstd[th][:, :], i=ps_ss[th][:, :]: e.activation(o, i, AF.Ln, bias=eps, scale=1.0 / D_MODEL)),
              reads=[ps_ss_t[th]], writes=[rstd_t[th]])
        s.add("act", (lambda e, o=rstd[th][:, :]: e.activation(o, o, AF.Exp, scale=-0.5)),
              reads=[rstd_t[th]], writes=[rstd_t[th]])
    for c in range(KC):
        for th in range(nth):
            sl = slice(th * 512, (th + 1) * 512)
            s.add("dve", (lambda e, o=h_sb[:, c, sl], i=x_sb[:, c, sl], g=g_sb[:, c:c + 1], r=rstd[th][:, :]:
                          e.scalar_tensor_tensor(o, i, g, r, ALU.mult, ALU.mult)),
                  reads=[xt[c][th], gt, rstd_t[th]], writes=[ht[c][th]])


def build_post():
    cx = Ctx()
    nc, s = cx.nc, cx.s
    xT = cx.dram_in("xT", [D_MODEL, TOK], F32)
    yT = cx.dram_in("yT", [D_MODEL, TOK], BF16)
    w = cx.dram_in("w", [D_MODEL, D_MODEL], F32)
    g = cx.dram_in("g", [KC, 128], F32)
    xo = cx.dram_out("xmT", [D_MODEL, TOK], F32)
    ho = cx.dram_out("h2T", [D_MODEL, TOK], BF16)

    x_sb = cx.sb("x_sb", [128, KC, TOK], F32)
    y_sb = cx.sb("y_sb", [128, KC, TOK], BF16)
    h_sb = cx.sb("h_sb", [128, KC, TOK], BF16)
    wb = [cx.sb(f"wb{i}", [128, KC, 512], BF16) for i in range(2)]
    g16 = cx.sb("g16", [KC, 128], F32)
    g_sb = cx.sb("g_sb", [128, KC], F32)
    ident = cx.sb("ident", [128, 128], F32)
    ones_f = cx.sb("ones_f", [128, 128], F32)
    sq = [cx.sb(f"sq{i}", [128, 512], F32) for i in range(2)]
    rstd = [cx.sb(f"rstd{i}", [128, 512], F32) for i in range(2)]
    pm = [cx.ps(f"pm{i}") for i in range(4)]
    pss = [cx.ps(f"pss{i}") for i in range(2)]
    pg = cx.ps("pg", [128, 128])

    xt = [[T(f"x{c}_{th}") for th in range(2)] for c in range(KC)]
    yt = [T(f"y{c}") for c in range(KC)]
    ht = [[T(f"h{c}_{th}") for th in range(2)] for c in range(KC)]
    wt = [T("wb0"), T("wb1")]
    pmt = [T(f"pm{i}") for i in range(4)]
    psst = [T("pss0"), T("pss1")]
    sqt = [T("sq0"), T("sq1")]
    rstdt = [T("r0"), T("r1")]
    gt, g16t, identt, onest, pgt = T("g"), T("g16"), T("ident"), T("ones"), T("pg")

    xTv = xT.rearrange("(c p) t -> p c t", p=128)
    yTv = yT.rearrange("(c p) t -> p c t", p=128)
    wv = w.rearrange("(c p) o -> p c o", p=128)
    xov = xo.rearrange("(c p) t -> p c t", p=128)
    hov = ho.rearrange("(c p) t -> p c t", p=128)

    s.add("pool", lambda e: e.memset(ones_f[:, :], 1.0), writes=[onest])
    emit_identity(cx, ident, identt)
    s.add("sp", lambda e: e.dma_start(out=g16[:, :], in_=g), writes=[g16t], dma="g")
    for q in range(2):
        s.add("sp", (lambda e, q=q: e.dma_start(out=y_sb[:, q * 8:(q + 1) * 8, :], in_=yTv[:, q * 8:(q + 1) * 8, :])),
              writes=[yt[c] for c in range(q * 8, (q + 1) * 8)], dma=f"y{q}")
    for q in range(4):
        s.add("sp", (lambda e, q=q: e.dma_start(out=x_sb[:, q * 4:(q + 1) * 4, :], in_=xTv[:, q * 4:(q + 1) * 4, :])),
              writes=[xt[c][th] for c in range(q * 4, (q + 1) * 4) for th in range(2)], dma=f"x{q}")
    s.add("pe", lambda e: e.transpose(pg[:, 0:KC], g16[:, :], ident[0:KC, 0:KC]), reads=[g16t, identt], writes=[pgt])
    s.add("dve", lambda e: e.tensor_copy(g_sb[:, :], pg[:, 0:KC]), reads=[pgt], writes=[gt])

    n = 0
    for cg in range(4):
        k = cg % 2
        s.add("pool", (lambda e, k=k, cg=cg: e.dma_start(out=wb[k][:, :, :], in_=wv[:, :, cg * 512:(cg + 1) * 512])),
              writes=[wt[k]], dma=f"w{k}")
        for oi in range(4):
            ot = cg * 4 + oi
            for th in range(2):
                sl = slice(th * 512, (th + 1) * 512)
                p = n % 4
                n += 1
                for kc in range(KC):
                    s.add("pe", _mm(pm[p][:, :], wb[k][:, kc, oi * 128:(oi + 1) * 128], y_sb[:, kc, sl], kc == 0, kc == KC - 1),
                          reads=[wt[k], yt[kc]], writes=[pmt[p]])
                s.add("dve", (lambda e, p=p, ot=ot, sl=sl: e.tensor_tensor(x_sb[:, ot, sl], pm[p][:, :], x_sb[:, ot, sl], ALU.add)),
                      reads=[pmt[p], xt[ot][th]], writes=[xt[ot][th]])
    emit_rmsnorm(cx, x_sb, xt, g_sb, gt, h_sb, ht, ones_f, onest, pss, psst, sq, sqt, rstd, rstdt, NORM_EPS)
    outs = []
    for c in range(KC):
        outs.append(s.add("sp", (lambda e, c=c: e.dma_start(out=xov[:, c, :], in_=x_sb[:, c, :])), reads=[xt[c][0], xt[c][1]], dma="xo"))
        outs.append(s.add("sp", (lambda e, c=c: e.dma_start(out=hov[:, c, :], in_=h_sb[:, c, :])), reads=[ht[c][0], ht[c][1]], dma="ho"))
    s.finish(outs)
    return cx.finish()


def emit_identity(cx, ident, identt):
    s = cx.s
    nc = cx.nc
    io = cx.sb("iota_tmp", [128, 128], F32)
    iot = T("iota_tmp")
    s.add("pool", lambda e: e.iota(io[:, :], [[1, 128]], base=0, channel_multiplier=-1, allow_small_or_imprecise_dtypes=True), writes=[iot])
    s.add("dve", lambda e: e.tensor_scalar(ident[:, :], io[:, :], 0.0, None, ALU.is_equal), reads=[iot], writes=[identt])


QKV_GROUPS = [("q", 0), ("q", 512), ("k", 1024), ("k", 1536), ("v", 0), ("v", 512),
              ("q", 2048), ("k", 2560), ("v", 1024), ("q", 3072), ("k", 3584), ("v", 1536)]


def emit_load_g(cx, g_dram, g_sb, gt, ident, identt, pg, pgt, tag):
    s = cx.s
    g16 = cx.sb(f"g16_{tag}", [KC, 128], F32)
    g16t = T(f"g16_{tag}")
    s.add("sp", lambda e: e.dma_start(out=g16[:, :], in_=g_dram), writes=[g16t], dma=f"g_{tag}")
    s.add("pe", lambda e: e.transpose(pg[:, 0:KC], g16[:, :], ident[0:KC, 0:KC]), reads=[g16t, identt], writes=[pgt])
    s.add("dve", lambda e: e.tensor_copy(g_sb[:, :], pg[:, 0:KC]), reads=[pgt], writes=[gt])


def build_proj():
    cx = Ctx()
    nc, s = cx.nc, cx.s
    xT = cx.dram_in("xT", [D_MODEL, TOK], F32)
    w = cx.dram_in("w", [D_MODEL, 6144], F32)
    g = cx.dram_in("g", [KC, 128], F32)
    qko = cx.dram_out("qkT", [4096, TOK], BF16)
    vo = cx.dram_out("vtok", [TOK, 2048], BF16)

    x_sb = cx.sb("x_sb", [128, KC, TOK], F32)
    h_sb = cx.sb("h_sb", [128, KC, TOK], BF16)
    wb = [cx.sb(f"wb{i}", [128, KC, 512], BF16) for i in range(2)]
    g_sb = cx.sb("g_sb", [128, KC], F32)
    ident = cx.sb("ident", [128, 128], F32)
    ones_f = cx.sb("ones_f", [128, 128], F32)
    sq = [cx.sb(f"sq{i}", [128, 512], F32) for i in range(2)]
    rstd = [cx.sb(f"rstd{i}", [128, 512], F32) for i in range(2)]
    stg = [cx.sb(f"stg{i}", [128, 512], BF16) for i in range(4)]
    pm = [cx.ps(f"pm{i}") for i in range(4)]
    pss = [cx.ps(f"pss{i}") for i in range(2)]
    pg = cx.ps("pg", [128, 128])

    xt = [[T(f"x{c}_{th}") for th in range(2)] for c in range(KC)]
    ht = [[T(f"h{c}_{th}") for th in range(2)] for c in range(KC)]
    wt = [T("wb0"), T("wb1")]
    pmt = [T(f"pm{i}") for i in range(4)]
    stgt = [T(f"stg{i}") for i in range(4)]
    psst = [T("pss0"), T("pss1")]
    sqt = [T("sq0"), T("sq1")]
    rstdt = [T("r0"), T("r1")]
    gt, identt, onest, pgt = T("g"), T("ident"), T("ones"), T("pg")

    xTv = xT.rearrange("(c p) t -> p c t", p=128)
    wv = w.rearrange("(c p) o -> p c o", p=128)
    qkv_ = qko.rearrange("(r p) t -> p r t", p=128)
    vov = vo.rearrange("(b p) f -> p b f", p=128)

    s.add("pool", lambda e: e.memset(ones_f[:, :], 1.0), writes=[onest])
    emit_identity(cx, ident, identt)
    for q in range(4):
        s.add("sp", (lambda e, q=q: e.dma_start(out=x_sb[:, q * 4:(q + 1) * 4, :], in_=xTv[:, q * 4:(q + 1) * 4, :])),
              writes=[xt[c][th] for c in range(q * 4, (q + 1) * 4) for th in range(2)], dma=f"x{q}")
    emit_load_g(cx, g, g_sb, gt, ident, identt, pg, pgt, "a")
    emit_rmsnorm(cx, x_sb, xt, g_sb, gt, h_sb, ht, ones_f, onest, pss, psst, sq, sqt, rstd, rstdt, NORM_EPS)
    outs = []
    n = 0
    for gi, (kind, base) in enumerate(QKV_GROUPS):
        k = gi % 2
        s.add("pool", (lambda e, k=k, gi=gi: e.dma_start(out=wb[k][:, :, :], in_=wv[:, :, gi * 512:(gi + 1) * 512])),
              writes=[wt[k]], dma=f"w{k}")
        if kind in ("q", "k"):
            for oi in range(4):
                for th in range(2):
                    sl = slice(th * 512, (th + 1) * 512)
                    p = n % 4
                    n += 1
                    for kc in range(KC):
                        s.add("pe", _mm(pm[p][:, :], wb[k][:, kc, oi * 128:(oi + 1) * 128], h_sb[:, kc, sl], kc == 0, kc == KC - 1),
                              reads=[wt[k], ht[kc][th]], writes=[pmt[p]])
                    sc = 0.125 if kind == "q" else 1.0
                    s.add("act", (lambda e, p=p, sc=sc: e.activation(stg[p][:, :], pm[p][:, :], AF.Copy, scale=sc)),
                          reads=[pmt[p]], writes=[stgt[p]])
                    r = base // 128 + oi
                    outs.append(s.add("sp", (lambda e, p=p, r=r, sl=sl: e.dma_start(out=qkv_[:, r, sl], in_=stg[p][:, :])),
                                      reads=[stgt[p]], dma=f"stg{p}"))
        else:
            for tb in range(TOK // 128):
                p = n % 4
                n += 1
                th = tb // 4
                for kc in range(KC):
                    s.add("pe", _mm(pm[p][:, :], h_sb[:, kc, tb * 128:(tb + 1) * 128], wb[k][:, kc, :], kc == 0, kc == KC - 1),
                          reads=[wt[k], ht[kc][th]], writes=[pmt[p]])
                s.add("act", (lambda e, p=p: e.activation(stg[p][:, :], pm[p][:, :], AF.Copy)),
                      reads=[pmt[p]], writes=[stgt[p]])
                outs.append(s.add("sp", (lambda e, p=p, tb=tb, base=base: e.dma_start(out=vov[:, tb, base:base + 512], in_=stg[p][:, :])),
                                  reads=[stgt[p]], dma=f"stg{p}"))
    s.finish(outs)
    return cx.finish()


def emit_load_T(cx, src_ap, R, dst_ap, dstt, ident, identt, pg, pgt, tag):
    s = cx.s
    tmp = cx.sb(f"ldT_{tag}", [R, 128], F32)
    tt = T(f"ldT_{tag}")
    s.add("sp", lambda e: e.dma_start(out=tmp[:, :], in_=src_ap), writes=[tt], dma=f"ldT_{tag}")
    s.add("pe", lambda e: e.transpose(pg[:, 0:R], tmp[:, :], ident[0:R, 0:R]), reads=[tt, identt], writes=[pgt])
    s.add("dve", lambda e: e.tensor_copy(dst_ap, pg[:, 0:R]), reads=[pgt], writes=[dstt])


NPAIR = FC // 2


def build_ffn(final):
    cx = Ctx()
    nc, s = cx.nc, cx.s
    hT = cx.dram_in("h2T", [D_MODEL, TOK + 2], BF16)
    xT = cx.dram_in("xmT", [D_MODEL, TOK], F32)
    wg = cx.dram_in("wg", [D_MODEL, D_FF], F32)
    wu = cx.dram_in("wu", [D_MODEL, D_FF], F32)
    wd = cx.dram_in("wd", [D_FF, D_MODEL], F32)
    cw = cx.dram_in("cw", [3 * FC, 128], F32)
    cb = cx.dram_in("cb", [FC, 128], F32)
    if final:
        gf = cx.dram_in("gf", [KC, 128], F32)
    xo = cx.dram_out("xoT", [D_MODEL, TOK], F32)

    x_sb = cx.sb("x_sb", [128, KC, TOK], F32)
    h_sb = cx.sb("h_sb", [128, KC, TOK + 2], BF16)
    wgb = [cx.sb(f"wgb{i}", [128, KC, 256], BF16) for i in range(2)]
    wub = [cx.sb(f"wub{i}", [128, KC, 256], BF16) for i in range(2)]
    wdb = [cx.sb(f"wdb{i}", [128, 2, D_MODEL], BF16) for i in range(2)]
    act = [cx.sb(f"act{i}", [128, 2, TOK], BF16) for i in range(2)]
    gp = [cx.sb(f"gp{i}", [128, TOK + 2], F32) for i in range(2)]
    av = [cx.sb(f"av{i}", [128, TOK], F32) for i in range(2)]
    sv = [cx.sb(f"sv{i}", [128, TOK], F32) for i in range(2)]
    cw_sb = cx.sb("cw_sb", [128, 3 * FC], F32)
    cb_sb = cx.sb("cb_sb", [128, FC], F32)
    ident = cx.sb("ident", [128, 128], F32)
    PG = [cx.ps(f"PG{i}") for i in range(2)]
    PU = [cx.ps(f"PU{i}") for i in range(2)]
    PH = cx.ps("PH")
    PD = [cx.ps(f"PD{i}") for i in range(3)]

    xt = [[T(f"x{c}_{th}") for th in range(2)] for c in range(KC)]
    ht = [T(f"h{c}") for c in range(KC)]
    wgt = [T("wg0"), T("wg1")]
    wut = [T("wu0"), T("wu1")]
    wdt = [T("wd0"), T("wd1")]
    actt = [[T(f"act{i}_{j}") for j in range(2)] for i in range(2)]
    gpt = [T("gp0"), T("gp1")]
    avt = [T("av0"), T("av1")]
    svt = [T("sv0"), T("sv1")]
    cwt, cbt, identt = T("cw"), T("cb"), T("ident")
    PGt = [T("PG0"), T("PG1")]
    PUt = [T("PU0"), T("PU1")]
    PHt = T("PH")
    PDt = [T(f"PD{i}") for i in range(3)]

    xTv = xT.rearrange("(c p) t -> p c t", p=128)
    hTv = hT.rearrange("(c p) t -> p c t", p=128)
    xov = xo.rearrange("(c p) t -> p c t", p=128)
    wgv = wg.rearrange("(c p) o -> p c o", p=128)
    wuv = wu.rearrange("(c p) o -> p c o", p=128)
    wdv = wd.rearrange("(f p) o -> p f o", p=128)

    emit_identity(cx, ident, identt)
    for q in range(2):
        s.add("sp", (lambda e, q=q: e.dma_start(out=h_sb[:, q * 8:(q + 1) * 8, :], in_=hTv[:, q * 8:(q + 1) * 8, :])),
              writes=[ht[c] for c in range(q * 8, (q + 1) * 8)], dma=f"h{q}")
    emit_load_T(cx, cw[0:88, :], 88, cw_sb[:, 0:88], cwt, ident, identt, PH, PHt, "cw0")
    emit_load_T(cx, cw[88:132, :], 44, cw_sb[:, 88:132], cwt, ident, identt, PH, PHt, "cw1")
    emit_load_T(cx, cb, 44, cb_sb[:, :], cbt, ident, identt, PH, PHt, "cb")
    for q in range(4):
        s.add("sp", (lambda e, q=q: e.dma_start(out=x_sb[:, q * 4:(q + 1) * 4, :], in_=xTv[:, q * 4:(q + 1) * 4, :])),
              writes=[xt[c][th] for c in range(q * 4, (q + 1) * 4) for th in range(2)], dma=f"x{q}")

    nd = [0]

    def down(pr):
        k = pr % 2
        for ot in range(KC):
            for th in range(2):
                sl = slice(th * 512, (th + 1) * 512)
                p = nd[0] % 3
                nd[0] += 1
                for j in range(2):
                    s.add("pe", _mm(PD[p][:, :], wdb[k][:, j, ot * 128:(ot + 1) * 128], act[k][:, j, sl], j == 0, j == 1),
                          reads=[wdt[k], actt[k][j]], writes=[PDt[p]])
                s.add("dve", (lambda e, p=p, ot=ot, sl=sl: e.tensor_tensor(x_sb[:, ot, sl], PD[p][:, :], x_sb[:, ot, sl], ALU.add)),
                      reads=[PDt[p], xt[ot][th]], writes=[xt[ot][th]])

    nf = 0
    for pr in range(NPAIR):
        k = pr % 2
        c0 = pr * 256
        s.add("pool", (lambda e, k=k, c0=c0: e.dma_start(out=wgb[k][:, :, :], in_=wgv[:, :, c0:c0 + 256])), writes=[wgt[k]], dma=f"wg{k}")
        s.add("pool", (lambda e, k=k, c0=c0: e.dma_start(out=wub[k][:, :, :], in_=wuv[:, :, c0:c0 + 256])), writes=[wut[k]], dma=f"wu{k}")
        s.add("pool", (lambda e, k=k, pr=pr: e.dma_start(out=wdb[k][:, :, :], in_=wdv[:, 2 * pr:2 * pr + 2, :])), writes=[wdt[k]], dma=f"wd{k}")
        for j in range(2):
            fc = 2 * pr + j
            b = nf % 2
            nf += 1
            wsl = slice(j * 128, (j + 1) * 128)
            for kc in range(KC):
                s.add("pe", _mm(PH[:, 0:2], wgb[k][:, kc, wsl], h_sb[:, kc, 0:2], kc == 0, kc == KC - 1),
                      reads=[wgt[k], ht[kc]], writes=[PHt])
            for th in range(2):
                for kc in range(KC):
                    s.add("pe", _mm(PG[th][:, :], wgb[k][:, kc, wsl], h_sb[:, kc, 2 + th * 512:2 + (th + 1) * 512], kc == 0, kc == KC - 1),
                          reads=[wgt[k], ht[kc]], writes=[PGt[th]])
            for th in range(2):
                for kc in range(KC):
                    s.add("pe", _mm(PU[th][:, :], wub[k][:, kc, wsl], h_sb[:, kc, 2 + th * 512:2 + (th + 1) * 512], kc == 0, kc == KC - 1),
                          reads=[wut[k], ht[kc]], writes=[PUt[th]])
            s.add("act", (lambda e, b=b: e.activation(gp[b][:, 0:2], PH[:, 0:2], AF.Copy)), reads=[PHt], writes=[gpt[b]])
            for th in range(2):
                s.add("act", (lambda e, b=b, th=th: e.activation(gp[b][:, 2 + th * 512:2 + (th + 1) * 512], PG[th][:, :], AF.Copy)),
                      reads=[PGt[th]], writes=[gpt[b]])
            s.add("dve", (lambda e, b=b, fc=fc: e.tensor_scalar(av[b][:, :], gp[b][:, 2:TOK + 2], cw_sb[:, 2 * FC + fc:2 * FC + fc + 1],
                                                                 cb_sb[:, fc:fc + 1], ALU.mult, ALU.add)),
                  reads=[gpt[b], cwt, cbt], writes=[avt[b]])
            s.add("dve", (lambda e, b=b, fc=fc: e.scalar_tensor_tensor(av[b][:, :], gp[b][:, 1:TOK + 1], cw_sb[:, FC + fc:FC + fc + 1],
                                                                        av[b][:, :], ALU.mult, ALU.add)),
                  reads=[gpt[b], cwt, avt[b]], writes=[avt[b]])
            s.add("dve", (lambda e, b=b, fc=fc: e.scalar_tensor_tensor(av[b][:, :], gp[b][:, 0:TOK], cw_sb[:, fc:fc + 1],
                                                                        av[b][:, :], ALU.mult, ALU.add)),
                  reads=[gpt[b], cwt, avt[b]], writes=[avt[b]])
            s.add("act", (lambda e, b=b: e.activation(sv[b][:, :], av[b][:, :], AF.Silu)), reads=[avt[b]], writes=[svt[b]])
            for th in range(2):
                sl = slice(th * 512, (th + 1) * 512)
                s.add("dve", (lambda e, b=b, k=k, j=j, th=th, sl=sl: e.tensor_tensor(act[k][:, j, sl], PU[th][:, :], sv[b][:, sl], ALU.mult)),
                      reads=[PUt[th], svt[b]], writes=[actt[k][j]])
        if pr >= 1:
            down(pr - 1)
    down(NPAIR - 1)
    outs = []
    if final:
        ones_f = cx.sb("ones_f", [128, 128], F32)
        onest = T("ones")
        s.add("pool", lambda e: e.memset(ones_f[:, :], 1.0), writes=[onest])
        gf_sb = cx.sb("gf_sb", [128, KC], F32)
        gft = T("gf")
        emit_load_T(cx, gf, KC, gf_sb[:, :], gft, ident, identt, PH, PHt, "gf")
        xt2 = xt
        sq = [cx.sb(f"sq{i}", [128, 512], F32) for i in range(2)]
        rstd = [cx.sb(f"rstd{i}", [128, 512], F32) for i in range(2)]
        emit_rmsnorm(cx, x_sb, xt2, gf_sb, gft, x_sb, xt2, ones_f, onest, PG, PGt, sq, [T("sq0"), T("sq1")],
                     rstd, [T("r0"), T("r1")], NORM_EPS)
    for c in range(KC):
        outs.append(s.add("sp", (lambda e, c=c: e.dma_start(out=xov[:, c, :], in_=x_sb[:, c, :])), reads=[xt[c][0], xt[c][1]], dma="xo"))
    s.finish(outs)
    return cx.finish()


NQT = SEQ // 512
NKB = SEQ // 128


def build_attn():
    cx = Ctx()
    nc, s = cx.nc, cx.s
    qa = cx.dram_in("qa", [2, 68, SEQ], BF16)
    ka = cx.dram_in("ka", [2, 68, SEQ], BF16)
    va = cx.dram_in("va", [128, NKB, 128], BF16)
    qb = cx.dram_in("qb", [64, SEQ], BF16)
    kb_ = cx.dram_in("kb", [64, SEQ], BF16)
    vb = cx.dram_in("vb", [128, NKB, 64], BF16)
    qc = cx.dram_in("qc", [64, SEQ], BF16)
    kc_ = cx.dram_in("kc", [64, SEQ], BF16)
    vc = cx.dram_in("vc", [128, NKB, 64], BF16)
    c16 = cx.dram_in("c16", [128, 6, 128], BF16)
    c32 = cx.dram_in("c32", [128, 4, 128], F32)
    vecs = cx.dram_in("vecs", [128, 8], F32)
    lamv = cx.dram_in("lamv", [128, 4, 64], F32)
    yo = cx.dram_out("yT", [256, SEQ], BF16)

    Qa = cx.sb("Qa", [128, 2, SEQ], BF16)
    Ka = cx.sb("Ka", [128, 2, SEQ], BF16)
    Va = cx.sb("Va", [128, NKB, 128], BF16)
    Qs = cx.sb("Qs", [64, SEQ], BF16)
    Ks = cx.sb("Ks", [64, SEQ], BF16)
    Vs = cx.sb("Vs", [128, NKB, 64], BF16)
    C16 = cx.sb("C16", [128, 6, 128], BF16)
    C32 = cx.sb("C32", [128, 4, 128], F32)
    VEC = cx.sb("VEC", [128, 8], F32)
    LAM = cx.sb("LAM", [128, 4, 64], F32)
    lt = cx.sb("lt", [128, 8], F32)
    ljunk = cx.sb("ljunk", [128, 64], F32)
    ones_f = cx.sb("ones_f", [128, 128], F32)
    Pt = [cx.sb(f"Pt{i}", [128, 512], BF16) for i in range(4)]
    Ef = [cx.sb(f"Ef{i}", [128, 512], F32) for i in range(2)]
    SPb = [cx.sb(f"SPb{i}", [128, 512], BF16) for i in range(2)]
    Lc32 = cx.sb("Lc32", [128, 512], F32)
    Lcb = cx.sb("Lcb", [128, 512], BF16)
    ev = [cx.sb(f"ev{i}", [128, 512], F32) for i in range(6)]
    yst = [cx.sb(f"yst{i}", [128, 512], BF16) for i in range(2)]
    B = [cx.ps(f"B{i}") for i in range(8)]

    Qat, Kat, Vat = [T("Qa0"), T("Qa1")], [T("Ka0"), T("Ka1")], T("Va")
    Qst, Kst, Vst = T("Qs"), T("Ks"), T("Vs")
    C16t, C32t, VECt, LAMt, ltt, onest = T("C16"), T("C32"), T("VEC"), T("LAM"), T("lt"), T("ones")
    Ptt = [T(f"Pt{i}") for i in range(4)]
    Eft = [T("Ef0"), T("Ef1")]
    SPt = [T("SP0"), T("SP1")]
    Lc32t, Lcbt = T("Lc32"), T("Lcb")
    evt = [T(f"ev{i}") for i in range(6)]
    ystt = [T("yst0"), T("yst1")]
    Bt = [T(f"B{i}") for i in range(8)]
    ljt = T("ljunk")

    IDB, ONESB, DCORR, NEGU, NEGONES, DNEG = range(6)
    M01, BT0, BT1, M4 = range(4)

    s.add("sp", lambda e: e.dma_start(out=C16[:, :, :], in_=c16), writes=[C16t], dma="c16")
    s.add("sp", lambda e: e.dma_start(out=C32[:, :, :], in_=c32), writes=[C32t], dma="c32")
    s.add("sp", lambda e: e.dma_start(out=VEC[:, :], in_=vecs), writes=[VECt], dma="vec")
    s.add("sp", lambda e: e.dma_start(out=LAM[:, :, :], in_=lamv), writes=[LAMt], dma="lam")
    for m in range(2):
        s.add("sp", (lambda e, m=m: e.dma_start(out=Ka[0:68, m, :], in_=ka[m])), writes=[Kat[m]], dma=f"ka{m}")
        s.add("sp", (lambda e, m=m: e.dma_start(out=Qa[0:68, m, :], in_=qa[m])), writes=[Qat[m]], dma=f"qa{m}")
    s.add("sp", lambda e: e.dma_start(out=Va[:, :, :], in_=va), writes=[Vat], dma="va")
    s.add("pool", lambda e: e.memset(ones_f[:, :], 1.0), writes=[onest])
    Zb = cx.sb("Zb", [128, 64], BF16)
    Zbt = T("Zb")
    s.add("pool", lambda e: e.memset(Zb[:, :], 0.0), writes=[Zbt])

    for i in range(2):
        s.add("dve", (lambda e, i=i: e.tensor_tensor(ljunk[:, :], LAM[:, 2 * i, :], LAM[:, 2 * i + 1, :], ALU.mult)),
              reads=[LAMt], writes=[ljt])
        s.add("dve", (lambda e, i=i: e.tensor_scalar(ljunk[:, :], ljunk[:, :], 1.0, 0.0, ALU.mult, ALU.add, accum_out=lt[:, i:i + 1])),
              reads=[ljt], writes=[ljt, ltt])
        s.add("act", (lambda e, i=i: e.activation(lt[:, 2 + i:3 + i], lt[:, i:i + 1], AF.Exp)), reads=[ltt], writes=[ltt])
    s.add("dve", lambda e: e.tensor_tensor(lt[:, 4:5], lt[:, 3:4], lt[:, 2:3], ALU.subtract), reads=[ltt], writes=[ltt])
    s.add("dve", lambda e: e.tensor_tensor(lt[:, 4:5], lt[:, 4:5], VEC[:, 0:1], ALU.subtract), reads=[ltt, VECt], writes=[ltt])
    s.add("dve", lambda e: e.tensor_tensor(lt[:, 5:6], VEC[:, 3:4], VEC[:, 1:2], ALU.mult), reads=[VECt], writes=[ltt])

    outs = []

    def colrange(t, kb):
        if kb >= 4 * t:
            c0 = 128 * (kb - 4 * t)
            return c0, 512 - c0, True
        return 0, 512, False

    def head_norm_out(np_, o_ap, ot_, g_ap, eps, row0, t, tagk):
        sqv, sqt_ = ev[5], evt[5]
        s.add("dve", lambda e: e.tensor_tensor(sqv[0:np_, :], o_ap, o_ap, ALU.mult), reads=[ot_], writes=[sqt_])
        s.add("pe", _mm(B[0][0:np_, :], ones_f[0:np_, 0:np_], sqv[0:np_, :], True, True), reads=[sqt_, onest], writes=[Bt[0]])
        s.add("act", lambda e: e.activation(sqv[0:np_, :], B[0][0:np_, :], AF.Ln, bias=eps, scale=1.0 / np_), reads=[Bt[0]], writes=[sqt_])
        s.add("act", lambda e: e.activation(sqv[0:np_, :], sqv[0:np_, :], AF.Exp, scale=-0.5), reads=[sqt_], writes=[sqt_])
        k = tagk % 2
        s.add("dve", lambda e: e.scalar_tensor_tensor(yst[k][0:np_, :], o_ap, g_ap, sqv[0:np_, :], ALU.mult, ALU.mult),
              reads=[ot_, sqt_, ltt, VECt], writes=[ystt[k]])
        outs.append(s.add("sp", lambda e: e.dma_start(out=yo[row0:row0 + np_, t * 512:(t + 1) * 512], in_=yst[k][0:np_, :]),
                          reads=[ystt[k]], dma=f"yst{k}"))

    nstep = 0
    for t in range(NQT):
        q0 = t * 512
        nkb = 4 * t + 4
        for kb in range(nkb):
            c0, N, diag = colrange(t, kb)
            for m in range(2):
                sb_i = (nstep % 2) * 2 + m
                S_, St_ = B[sb_i], Bt[sb_i]
                P_, Pt_ = Pt[sb_i], Ptt[sb_i]
                s.add("pe", _mm(S_[:, c0:512], Ka[0:68, m, kb * 128:(kb + 1) * 128], Qa[0:68, m, q0 + c0:q0 + 512], True, not diag),
                      reads=[Kat[m], Qat[m]], writes=[St_])
                if diag:
                    s.add("pe", _mm(S_[:, c0:c0 + 128], C16[:, IDB, :], C16[:, DCORR, :], False, True), reads=[C16t], writes=[St_])
                s.add("act", (lambda e, P_=P_, S_=S_, c0=c0: e.activation(P_[:, c0:512], S_[:, c0:512], AF.Exp)),
                      reads=[St_], writes=[Pt_])
                s.add("pe", _mm(B[4 + m][:, c0:512], Va[:, kb, :], P_[:, c0:512], kb == 0, kb == nkb - 1), reads=[Vat, Pt_], writes=[Bt[4 + m]])
                s.add("pe", _mm(B[6 + m][:, c0:512], C16[:, ONESB, :], P_[:, c0:512], kb == 0, kb == nkb - 1), reads=[C16t, Pt_], writes=[Bt[6 + m]])
            nstep += 1
        for i in range(4):
            s.add("act", (lambda e, i=i: e.activation(ev[i][:, :], B[4 + i][:, :], AF.Copy)), reads=[Bt[4 + i]], writes=[evt[i]])
        for m in range(2):
            s.add("dve", (lambda e, m=m: e.reciprocal(ev[2 + m][:, :], ev[2 + m][:, :])), reads=[evt[2 + m]], writes=[evt[2 + m]])
            s.add("dve", (lambda e, m=m: e.tensor_tensor(ev[m][:, :], ev[m][:, :], ev[2 + m][:, :], ALU.mult)),
                  reads=[evt[m], evt[2 + m]], writes=[evt[m]])
        s.add("dve", lambda e: e.scalar_tensor_tensor(ev[4][:, :], ev[1][:, :], lt[:, 4:5], ev[0][:, :], ALU.mult, ALU.add),
              reads=[evt[0], evt[1], ltt], writes=[evt[4]])
        head_norm_out(128, ev[4][:, :], evt[4], lt[:, 5:6], SUBLN_EPS, 0, t, t)

    s.add("sp", lambda e: e.dma_start(out=Ks[:, :], in_=kb_), writes=[Kst], dma="ks")
    s.add("sp", lambda e: e.dma_start(out=Qs[:, :], in_=qb), writes=[Qst], dma="qs")
    s.add("sp", lambda e: e.dma_start(out=Vs[:, :, :], in_=vb), writes=[Vst], dma="vs")
    nstep = 0
    for t in range(NQT):
        q0 = t * 512
        nkb = 4 * t + 4
        OB, OBt = B[4 + t % 2], Bt[4 + t % 2]
        s.add("pe", _mm(OB[0:64, :], Zb[0:64, 0:64], Qs[0:64, q0:q0 + 512], True, False), reads=[Zbt, Qst], writes=[OBt])
        for kb in range(nkb - 1, -1, -1):
            c0, N, diag = colrange(t, kb)
            first = kb == nkb - 1
            k2 = nstep % 2
            nstep += 1
            Z_, Zt_ = B[k2], Bt[k2]
            A_, At_ = B[2 + k2], Bt[2 + k2]
            Kblk = Ks[:, kb * 128:(kb + 1) * 128]
            Qcols = Qs[:, q0 + c0:q0 + 512]
            s.add("pe", _mm(Z_[:, c0:512], Kblk, Qcols, True, True), reads=[Kst, Qst], writes=[Zt_])
            s.add("act", (lambda e, k2=k2, Z_=Z_, c0=c0: e.activation(Ef[k2][:, c0:512], Z_[:, c0:512], AF.Exp)), reads=[Zt_], writes=[Eft[k2]])
            s.add("act", (lambda e, k2=k2, c0=c0: e.activation(SPb[k2][:, c0:512], Ef[k2][:, c0:512], AF.Ln, bias=1.0)),
                  reads=[Eft[k2]], writes=[SPt[k2]])
            if diag:
                s.add("dve", (lambda e, k2=k2, c0=c0: e.tensor_tensor(SPb[k2][:, c0:c0 + 128], SPb[k2][:, c0:c0 + 128], C32[:, M01, :], ALU.mult)),
                      reads=[SPt[k2], C32t], writes=[SPt[k2]])
            s.add("pe", _mm(A_[:, c0:512], Kblk, Qcols, True, False), reads=[Kst, Qst], writes=[At_])
            c1 = c0 + 128 if diag else 0
            has_carry = (not first) and c1 < 512
            s.add("pe", _mm(A_[:, c0:512], C16[:, NEGU, :], SPb[k2][:, c0:512], False, not (diag or has_carry)),
                  reads=[C16t, SPt[k2]], writes=[At_])
            if has_carry:
                s.add("pe", _mm(A_[:, c1:512], C16[:, NEGONES, :], Lcb[:, c1:512], False, not diag), reads=[C16t, Lcbt], writes=[At_])
            if diag:
                s.add("pe", _mm(A_[:, c0:c0 + 128], C16[:, IDB, :], C16[:, DNEG, :], False, True), reads=[C16t], writes=[At_])
            P_, Pt_ = Pt[k2], Ptt[k2]
            s.add("act", (lambda e, P_=P_, A_=A_, c0=c0: e.activation(P_[:, c0:512], A_[:, c0:512], AF.Exp)), reads=[At_], writes=[Pt_])
            s.add("pe", _mm(OB[0:64, c0:512], Vs[:, kb, :], P_[:, c0:512], False, kb == 0), reads=[Vst, Pt_], writes=[OBt])
            if kb > 0:
                if diag:
                    s.add("dve", (lambda e, k2=k2, c0=c0: e.tensor_copy(Lc32[:, c0:c0 + 128], SPb[k2][:, c0:c0 + 128])),
                          reads=[SPt[k2]], writes=[Lc32t])
                    if c0 + 128 < 512:
                        s.add("dve", (lambda e, k2=k2, c0=c0: e.tensor_tensor(Lc32[:, c0 + 128:512], Lc32[:, c0 + 128:512], SPb[k2][:, c0 + 128:512], ALU.add)),
                              reads=[SPt[k2], Lc32t], writes=[Lc32t])
                else:
                    s.add("dve", (lambda e, k2=k2: e.tensor_tensor(Lc32[:, :], Lc32[:, :], SPb[k2][:, :], ALU.add)),
                          reads=[SPt[k2], Lc32t], writes=[Lc32t])
                s.add("dve", (lambda e, c0=c0: e.tensor_copy(Lcb[:, c0:512], Lc32[:, c0:512])), reads=[Lc32t], writes=[Lcbt])
        s.add("act", (lambda e, OB=OB: e.activation(ev[0][0:64, :], OB[0:64, :], AF.Copy)), reads=[OBt], writes=[evt[0]])
        head_norm_out(64, ev[0][0:64, :], evt[0], VEC[0:64, 4:5], NORM_EPS, 128, t, t)

    s.add("sp", lambda e: e.dma_start(out=Ks[:, :], in_=kc_), writes=[Kst], dma="ks")
    s.add("sp", lambda e: e.dma_start(out=Qs[:, :], in_=qc), writes=[Qst], dma="qs")
    s.add("sp", lambda e: e.dma_start(out=Vs[:, :, :], in_=vc), writes=[Vst], dma="vs")
    nstep = 0
    for t in range(NQT):
        q0 = t * 512
        OB, OBt = B[4 + t % 2], Bt[4 + t % 2]
        RB, RBt = B[6 + t % 2], Bt[6 + t % 2]
        started = False
        last_d = 4 if t >= 1 else 3
        for d in range(5):
            j0 = max(0, d - 4 * t)
            if j0 >= 4:
                continue
            c0 = 128 * j0
            k2 = nstep % 2
            nstep += 1
            S_, St_ = B[k2], Bt[k2]
            for j in range(j0, 4):
                kbi = 4 * t + j - d
                s.add("pe", _mm(S_[:, j * 128:(j + 1) * 128], Ks[:, kbi * 128:(kbi + 1) * 128], Qs[:, q0 + j * 128:q0 + (j + 1) * 128], j == j0, j == 3),
                      reads=[Kst, Qst], writes=[St_])
            src, srct = S_, St_
            if d in (0, 1, 4):
                bi = {0: BT0, 1: BT1, 4: M4}[d]
                for j in range(j0, 4):
                    s.add("dve", (lambda e, k2=k2, S_=S_, j=j, bi=bi: e.tensor_tensor(Ef[k2][:, j * 128:(j + 1) * 128], S_[:, j * 128:(j + 1) * 128], C32[:, bi, :], ALU.add)),
                          reads=[St_, C32t], writes=[Eft[k2]])
                src, srct = Ef[k2], Eft[k2]
            P_, Pt_ = Pt[k2], Ptt[k2]
            if d >= 2:
                s.add("act", (lambda e, P_=P_, src=src, c0=c0: e.activation(P_[:, c0:512], src[:, c0:512], AF.Exp, bias=VEC[:, 2:3])),
                      reads=[srct, VECt], writes=[Pt_])
            else:
                s.add("act", (lambda e, P_=P_, src=src, c0=c0: e.activation(P_[:, c0:512], src[:, c0:512], AF.Exp)), reads=[srct], writes=[Pt_])
            for j in range(j0, 4):
                kbi = 4 * t + j - d
                st_flag = not started
                started = True
                s.add("pe", _mm(OB[0:64, j * 128:(j + 1) * 128], Vs[:, kbi, :], P_[:, j * 128:(j + 1) * 128], st_flag, d == last_d and j == 3), reads=[Vst, Pt_], writes=[OBt])
            s.add("pe", _mm(RB[0:64, c0:512], C16[:, ONESB, 0:64], P_[:, c0:512], d == 0, d == last_d), reads=[C16t, Pt_], writes=[RBt])
        s.add("act", (lambda e, OB=OB: e.activation(ev[0][0:64, :], OB[0:64, :], AF.Copy)), reads=[OBt], writes=[evt[0]])
        s.add("act", (lambda e, RB=RB: e.activation(ev[2][0:64, :], RB[0:64, :], AF.Copy)), reads=[RBt], writes=[evt[2]])
        s.add("dve", lambda e: e.reciprocal(ev[2][0:64, :], ev[2][0:64, :]), reads=[evt[2]], writes=[evt[2]])
        s.add("dve", lambda e: e.tensor_tensor(ev[0][0:64, :], ev[0][0:64, :], ev[2][0:64, :], ALU.mult), reads=[evt[0], evt[2]], writes=[evt[0]])
        head_norm_out(64, ev[0][0:64, :], evt[0], VEC[0:64, 5:6], NORM_EPS, 192, t, t)

    s.finish(outs)
    return cx.finish()


NEG = -30000.0


def attn_consts(h, rel_tab):
    slope = 2.0 ** (-(h + 1))
    ki = np.arange(128)[:, None]
    qi = np.arange(128)[None, :]
    identb = np.eye(128, dtype=np.float32)
    onesb = np.ones((128, 128), np.float32)
    dcorr = np.where((ki >= 64) & (qi < 64), NEG, np.where(ki > qi, -2.0 * slope * (ki - qi), 0.0)).astype(np.float32)
    negU = np.where(ki >= qi, -1.0, 0.0).astype(np.float32)
    negOnes = -onesb
    dneg = np.where(ki < qi, 0.0, NEG).astype(np.float32)
    c16 = np.stack([identb, onesb, dcorr, negU, negOnes, dneg], 1).astype(NPBF16)
    m01 = (ki < qi).astype(np.float32)
    idx0 = np.clip(qi - ki, -63, 128) + 63
    bt0 = np.where((ki >= 64) & (qi < 64), np.float32(NEG), rel_tab[idx0]).astype(np.float32)
    idx1 = np.minimum(128 + qi - ki, 128) + 63
    bt1 = rel_tab[idx1].astype(np.float32)
    m4 = np.where((qi >= 64) & (ki < 64), NEG, 0.0).astype(np.float32)
    c32 = np.stack([m01, bt0, bt1, m4], 1).astype(np.float32)
    return np.ascontiguousarray(c16), np.ascontiguousarray(c32)


def attn_pos_rows(h):
    slope = 2.0 ** (-(h + 1))
    t = np.arange(SEQ)
    blk = (t // 128).astype(np.float32)
    ti = (t % 128).astype(np.float32)
    one = np.ones(SEQ, np.float32)
    qrows = np.stack([one, one, -slope * 128.0 * blk, -slope * ti], 0).astype(NPBF16)
    krows = np.stack([slope * 128.0 * blk, slope * ti, one, one], 0).astype(NPBF16)
    return qrows, krows


def attn_in_maps(qkT, v, P, l):
    lam_init = 0.8 - 0.6 * math.exp(-0.3 * l)
    maps = []
    for h in range(NCORES):
        qrows, krows = attn_pos_rows(h)
        qa = np.empty((2, 68, SEQ), NPBF16)
        ka = np.empty((2, 68, SEQ), NPBF16)
        for m in range(2):
            qa[m, 0:64] = qkT[h * 128 + m * 64:h * 128 + (m + 1) * 64]
            qa[m, 64:68] = qrows
            ka[m, 0:64] = qkT[1024 + h * 128 + m * 64:1024 + h * 128 + (m + 1) * 64]
            ka[m, 64:68] = krows

        def vblk(cols):
            return np.ascontiguousarray(v[:, cols].reshape(NKB, 128, -1).transpose(1, 0, 2))
        c16, c32 = attn_consts(h, P["rel_bias"][l, h])
        vecs = np.zeros((128, 8), np.float32)
        vecs[:, 0] = lam_init
        vecs[:, 1] = 1.0 - lam_init
        vecs[:, 2] = P["rel_bias"][l, h, 191]
        vecs[:, 3] = P["diff_subln"][l]
        vecs[:, 4] = np.tile(P["sb_norm"][l, h], 2)
        vecs[:, 5] = np.tile(P["ch_norm"][l, h], 2)
        lamv = np.stack([np.broadcast_to(P[k][l], (128, 64)) for k in ("lambda_q1", "lambda_k1", "lambda_q2", "lambda_k2")], 1)
        maps.append({
            "qa": qa, "ka": ka, "va": vblk(slice(h * 128, (h + 1) * 128)),
            "qb": np.ascontiguousarray(qkT[2048 + h * 64:2048 + (h + 1) * 64]),
            "kb": np.ascontiguousarray(qkT[2560 + h * 64:2560 + (h + 1) * 64]),
            "vb": vblk(slice(1024 + h * 64, 1024 + (h + 1) * 64)),
            "qc": np.ascontiguousarray(qkT[3072 + h * 64:3072 + (h + 1) * 64]),
            "kc": np.ascontiguousarray(qkT[3584 + h * 64:3584 + (h + 1) * 64]),
            "vc": vblk(slice(1536 + h * 64, 1536 + (h + 1) * 64)),
            "c16": c16, "c32": c32, "vecs": vecs, "lamv": np.ascontiguousarray(lamv.astype(np.float32)),
        })
    return maps


_PROGS = {}


def _prog(name):
    if name not in _PROGS:
        _PROGS[name] = {"proj": build_proj, "attn": build_attn, "post": build_post,
                        "ffn": lambda: build_ffn(False), "ffn_final": lambda: build_ffn(True)}[name]()
    return _PROGS[name]


def _run(name, in_maps):
    res = run_bass_kernel_spmd(_prog(name), in_maps, core_ids=list(range(NCORES)))
    return res.results


def kernel(x, attn_norm, w_qkv, lambda_q1, lambda_k1, lambda_q2, lambda_k2, diff_subln, sb_norm, rel_bias,
           ch_norm, w_o, ffn_norm, w_gate, w_up, conv_w, conv_b, w_down, final_norm):
    f32 = lambda a: np.ascontiguousarray(np.asarray(a, dtype=np.float32))
    P = {"lambda_q1": f32(lambda_q1), "lambda_k1": f32(lambda_k1), "lambda_q2": f32(lambda_q2), "lambda_k2": f32(lambda_k2),
         "diff_subln": f32(diff_subln), "sb_norm": f32(sb_norm), "rel_bias": f32(rel_bias), "ch_norm": f32(ch_norm)}
    x = f32(x)[0]
    attn_norm, w_qkv, w_o, ffn_norm = f32(attn_norm), f32(w_qkv), f32(w_o), f32(ffn_norm)
    w_gate, w_up, w_down, conv_w, conv_b, final_norm = f32(w_gate), f32(w_up), f32(w_down), f32(conv_w), f32(conv_b), f32(final_norm)
    xT = [np.ascontiguousarray(x[c * TOK:(c + 1) * TOK].T) for c in range(NCORES)]
    for l in range(DEPTH):
        r = _run("proj", [{"xT": xT[c], "w": w_qkv[l], "g": attn_norm[l].reshape(KC, 128)} for c in range(NCORES)])
        qkT = np.concatenate([np.asarray(r[c]["qkT"]) for c in range(NCORES)], axis=1)
        v = np.concatenate([np.asarray(r[c]["vtok"]) for c in range(NCORES)], axis=0)
        r = _run("attn", attn_in_maps(qkT, v, P, l))
        yT = np.empty((D_MODEL, SEQ), NPBF16)
        for h in range(NCORES):
            yh = np.asarray(r[h]["yT"])
            yT[h * 128:(h + 1) * 128] = yh[0:128]
            yT[1024 + h * 64:1024 + (h + 1) * 64] = yh[128:192]
            yT[1536 + h * 64:1536 + (h + 1) * 64] = yh[192:256]
        r = _run("post", [{"xT": xT[c], "yT": np.ascontiguousarray(yT[:, c * TOK:(c + 1) * TOK]), "w": w_o[l],
                           "g": ffn_norm[l].reshape(KC, 128)} for c in range(NCORES)])
        xmT = [np.asarray(r[c]["xmT"]) for c in range(NCORES)]
        h2 = np.concatenate([np.zeros((D_MODEL, 2), NPBF16)] + [np.asarray(r[c]["h2T"]) for c in range(NCORES)], axis=1)
        final = l == DEPTH - 1
        maps = []
        for c in range(NCORES):
            m = {"h2T": np.ascontiguousarray(h2[:, c * TOK:(c + 1) * TOK + 2]), "xmT": xmT[c], "wg": w_gate[l], "wu": w_up[l],
                 "wd": w_down[l], "cw": conv_w[l].reshape(3 * FC, 128), "cb": conv_b[l].reshape(FC, 128)}
            if final:
                m["gf"] = final_norm.reshape(KC, 128)
            maps.append(m)
        r = _run("ffn_final" if final else "ffn", maps)
        xT = [np.asarray(r[c]["xoT"]) for c in range(NCORES)]
    out = np.concatenate([xT[c].T for c in range(NCORES)], axis=0)
    return np.ascontiguousarray(out[None].astype(np.float32))
```
